# Optimizing a Trainium2 kernel written in Bass

```python
import jax, jax.numpy as jnp
from jax import lax
import numpy as np

D_MODEL = 1024
BATCH = 16
SEQ = 2048
DEPTH = 2

GRID_W = 64
CTX_LEN = 256
EPS = 1e-6
ROPE_THETA = 10000.0
Q_BLOCK = 128
N_MOD = 6
N_BRANCH = 3
D_FF = 4 * D_MODEL
A_HEADS = 8
A_KV_HEADS = 2
A_GROUP = A_HEADS // A_KV_HEADS
A_HEAD_DIM = 64
B_HEADS = 4
B_KEY_DIM = 64
B_VAL_DIM = 128
B_GATE_RANK = 16
B_GATE_NORM = 16.0
B_CHUNK = 64
C_HEADS = 8
C_NOPE = 64
C_ROPE = 32
C_VDIM = 64
C_Q_RANK = 384
C_KV_RANK = 256
BRANCH_W = 512
IN_SPLITS = (A_HEADS * A_HEAD_DIM, A_KV_HEADS * A_HEAD_DIM, A_KV_HEADS * A_HEAD_DIM,
             B_HEADS * B_KEY_DIM, B_HEADS * B_KEY_DIM, B_HEADS * B_VAL_DIM, 2 * B_GATE_RANK, B_HEADS * B_VAL_DIM,
             C_Q_RANK, C_KV_RANK, C_ROPE, N_BRANCH * D_MODEL)
D_IN = sum(IN_SPLITS)

kernel_name = 'hybrid_parallel_mixer_dit_prefix'


def rms_norm(x, g):
    xf = x.astype(jnp.float32)
    y = xf * lax.rsqrt(jnp.mean(xf * xf, axis=-1, keepdims=True) + EPS)
    return (y * g.astype(jnp.float32)).astype(x.dtype)


def modulate(h, shift, scale):
    return h * (1.0 + scale) + shift


def axial_rope_tables(rows, rot_dim, dtype):
    row = jnp.repeat(jnp.arange(rows, dtype=jnp.float32), GRID_W)
    col = jnp.tile(jnp.arange(GRID_W, dtype=jnp.float32), rows)
    n_freq = rot_dim // 4
    inv_freq = ROPE_THETA ** (-jnp.arange(n_freq, dtype=jnp.float32) / n_freq)
    ang = jnp.concatenate([row[:, None] * inv_freq, col[:, None] * inv_freq], axis=-1)
    return jnp.cos(ang).astype(dtype), jnp.sin(ang).astype(dtype)


def apply_rope(x, cos, sin):
    xp = x.reshape(x.shape[:-1] + (-1, 2))
    x0, x1 = xp[..., 0], xp[..., 1]
    out = jnp.stack([x0 * cos - x1 * sin, x0 * sin + x1 * cos], axis=-1)
    return out.reshape(x.shape).astype(x.dtype)


def split_in(z):
    idx = np.cumsum(np.array(IN_SPLITS))[:-1].tolist()
    return jnp.split(z, idx, axis=-1)


def block_attention(q, k, v, scale):
    B, KVH, G, N, d = q.shape
    nb = N // Q_BLOCK
    qb = q.reshape(B, KVH, G, nb, Q_BLOCK, d).transpose(3, 0, 1, 2, 4, 5)

    def one_block(qi):
        s = jnp.einsum('bkgqd,bkmd->bkgqm', qi, k).astype(jnp.float32) * scale
        p = jax.nn.softmax(s, axis=-1).astype(v.dtype)
        return jnp.einsum('bkgqm,bkmd->bkgqd', p, v)

    o = lax.map(one_block, qb)
    return o.transpose(1, 2, 3, 0, 4, 5).reshape(B, KVH, G, N, v.shape[-1])


def merge_heads(o):
    B, K, G, N, d = o.shape
    return o.transpose(0, 3, 1, 2, 4).reshape(B, N, K * G * d)


def gqa_mixer(pl, pc, g_q, g_k, cos, sin, need_ctx):
    def heads(q, k, v):
        B, N, _ = q.shape
        q = q.reshape(B, N, A_KV_HEADS, A_GROUP, A_HEAD_DIM).transpose(0, 2, 3, 1, 4)
        k = k.reshape(B, N, A_KV_HEADS, A_HEAD_DIM).transpose(0, 2, 1, 3)
        v = v.reshape(B, N, A_KV_HEADS, A_HEAD_DIM).transpose(0, 2, 1, 3)
        return rms_norm(q, g_q), rms_norm(k, g_k), v

    q_l, k_l, v_l = heads(*pl)
    q_c, k_c, v_c = heads(*pc)
    q_l = apply_rope(q_l, cos, sin)
    k_l = apply_rope(k_l, cos, sin)
    k_all = jnp.concatenate([k_c, k_l], axis=2)
    v_all = jnp.concatenate([v_c, v_l], axis=2)
    scale = A_HEAD_DIM ** -0.5
    o_l = merge_heads(block_attention(q_l, k_all, v_all, scale))
    o_c = merge_heads(block_attention(q_c, k_c, v_c, scale)) if need_ctx else None
    return o_l, o_c


def gla_scan(q, k, v, log_a, s0):
    B, H, N, dk = q.shape
    dv = v.shape[-1]
    nc = N // B_CHUNK
    r = lambda t: t.reshape(B, H, nc, B_CHUNK, t.shape[-1])
    q, k, v, log_a = r(q), r(k), r(v), r(log_a)
    b = jnp.cumsum(log_a, axis=3)
    b_last = b[:, :, :, -1:, :]
    q_dec = q * jnp.exp(b)
    k_dec = k * jnp.exp(-b)
    k_to_end = k * jnp.exp(b_last - b)
    mask = jnp.tril(jnp.ones((B_CHUNK, B_CHUNK), dtype=bool))
    attn = jnp.where(mask, jnp.einsum('bhcid,bhcjd->bhcij', q_dec, k_dec), 0.0)
    o_intra = jnp.einsum('bhcij,bhcjv->bhciv', attn, v)
    u = jnp.einsum('bhcjd,bhcjv->bhcdv', k_to_end, v)
    g = jnp.exp(b_last[:, :, :, 0, :])

    def step(s, inp):
        g_c, u_c = inp
        return g_c[..., None] * s + u_c, s

    s_final, s_enter = lax.scan(step, s0, (g.transpose(2, 0, 1, 3), u.transpose(2, 0, 1, 3, 4)))
    s_enter = s_enter.transpose(1, 2, 0, 3, 4)
    o_inter = jnp.einsum('bhcid,bhcdv->bhciv', q_dec, s_enter)
    return (o_intra + o_inter).reshape(B, H, N, dv), s_final


def gla_bidir(q, k, v, la_f, la_b, s0_f, s0_b):
    o_f, s_f = gla_scan(q, k, v, la_f, s0_f)
    flip = lambda t: jnp.flip(t, axis=2)
    o_b, s_b = gla_scan(flip(q), flip(k), flip(v), flip(la_b), s0_b)
    return o_f + flip(o_b), s_f, s_b


def gla_mixer(pl, pc, w_alpha, b_alpha, g_out):
    def prep(q, k, v, a_lr):
        B, N, _ = q.shape
        heads = lambda t: t.reshape(B, N, B_HEADS, -1).transpose(0, 2, 1, 3).astype(jnp.float32)
        logits = jnp.einsum('bnzr,zrd->zbnd', a_lr.reshape(B, N, 2, B_GATE_RANK), w_alpha) + b_alpha[:, None, None, :]
        log_a = jax.nn.log_sigmoid(logits.astype(jnp.float32)) / B_GATE_NORM
        return heads(q) * (B_KEY_DIM ** -0.5), heads(k), heads(v), heads(log_a[0]), heads(log_a[1])

    q_c, k_c, v_c, lf_c, lb_c = prep(*pc[:4])
    s0 = jnp.zeros((q_c.shape[0], B_HEADS, B_KEY_DIM, B_VAL_DIM), jnp.float32)
    o_c, s_f, s_b = gla_bidir(q_c, k_c, v_c, lf_c, lb_c, s0, s0)
    q_l, k_l, v_l, lf_l, lb_l = prep(*pl[:4])
    o_l, _, _ = gla_bidir(q_l, k_l, v_l, lf_l, lb_l, s_f, s_b)

    def finish(o, gate):
        o = rms_norm(o.transpose(0, 2, 1, 3), g_out)
        B, N = o.shape[:2]
        return (o.reshape(B, N, -1) * jax.nn.silu(gate.astype(jnp.float32))).astype(gate.dtype)

    return finish(o_l, pl[4]), finish(o_c, pc[4])


def mla_mixer(pl, pc, g_cq, g_ckv, w_uq, w_ukv, cos, sin, need_ctx):
    def heads(cq, ckv, kr):
        B, N, _ = cq.shape
        q = (rms_norm(cq, g_cq) @ w_uq).reshape(B, N, C_HEADS, C_NOPE + C_ROPE).transpose(0, 2, 1, 3)
        kv = (rms_norm(ckv, g_ckv) @ w_ukv).reshape(B, N, C_HEADS, C_NOPE + C_VDIM).transpose(0, 2, 1, 3)
        return q, kv[..., :C_NOPE], kv[..., C_NOPE:], kr[:, None]

    def full_key(k_nope, k_rope):
        return jnp.concatenate([k_nope, jnp.broadcast_to(k_rope, k_nope.shape[:-1] + (C_ROPE,))], axis=-1)

    q_l, kn_l, v_l, kr_l = heads(*pl)
    q_c, kn_c, v_c, kr_c = heads(*pc)
    q_l = jnp.concatenate([q_l[..., :C_NOPE], apply_rope(q_l[..., C_NOPE:], cos, sin)], axis=-1)
    k_l = full_key(kn_l, apply_rope(kr_l, cos, sin))
    k_c = full_key(kn_c, kr_c)
    k_all = jnp.concatenate([k_c, k_l], axis=2)
    v_all = jnp.concatenate([v_c, v_l], axis=2)
    scale = (C_NOPE + C_ROPE) ** -0.5
    o_l = merge_heads(block_attention(q_l[:, :, None], k_all, v_all, scale))
    o_c = merge_heads(block_attention(q_c[:, :, None], k_c, v_c, scale)) if need_ctx else None
    return o_l, o_c


def merge_branches(outs, gate_logits, w_branch, w_out):
    B, N, _ = gate_logits.shape
    gates = jax.nn.sigmoid(gate_logits).reshape(B, N, N_BRANCH, D_MODEL)
    branch = jnp.stack(outs, axis=2)
    proj = jnp.einsum('bnzw,zwd->bnzd', branch, w_branch)
    return jnp.sum(gates * proj, axis=2) @ w_out


def mixing_sublayer(h_lat, h_ctx, w_in, a_g_q, a_g_k, b_w_alpha, b_b_alpha, b_g_out,
                    c_g_q, c_g_kv, c_w_uq, c_w_ukv, w_branch, w_out, rope_a, rope_c, need_ctx):
    pl = split_in(h_lat @ w_in)
    pc = split_in(h_ctx @ w_in)
    oa_l, oa_c = gqa_mixer(pl[0:3], pc[0:3], a_g_q, a_g_k, rope_a[0], rope_a[1], need_ctx)
    ob_l, ob_c = gla_mixer(pl[3:8], pc[3:8], b_w_alpha, b_b_alpha, b_g_out)
    oc_l, oc_c = mla_mixer(pl[8:11], pc[8:11], c_g_q, c_g_kv, c_w_uq, c_w_ukv, rope_c[0], rope_c[1], need_ctx)
    y_lat = merge_branches((oa_l, ob_l, oc_l), pl[11], w_branch, w_out)
    y_ctx = merge_branches((oa_c, ob_c, oc_c), pc[11], w_branch, w_out) if need_ctx else None
    return y_lat, y_ctx


def sq_relu_mlp(h, w1, w2):
    return jnp.square(jax.nn.relu(h @ w1)) @ w2


def setup_inputs(seed: int = 0) -> dict:
    key = jax.random.key(seed)
    ks = jax.random.split(key, 24)
    L = DEPTH

    def nrm(k, shape, scale):
        return jax.random.normal(k, shape, jnp.float32) * scale

    def gain(k, shape):
        return 1.0 + 0.05 * jax.random.normal(k, shape, jnp.float32)

    return {
        'x': nrm(ks[0], (BATCH, SEQ, D_MODEL), 1.0),
        'c': nrm(ks[1], (BATCH, D_MODEL), 1.0),
        'ctx': nrm(ks[2], (BATCH, CTX_LEN, D_MODEL), 1.0),
        'c_ctx': nrm(ks[3], (D_MODEL,), 1.0),
        'w_ada': nrm(ks[4], (L, D_MODEL, N_MOD * D_MODEL), 0.5 * D_MODEL ** -0.5),
        'b_ada': nrm(ks[5], (L, N_MOD * D_MODEL), 0.01),
        'g_pre_attn': gain(ks[6], (L, D_MODEL)),
        'g_post_attn': gain(ks[7], (L, D_MODEL)),
        'g_pre_mlp': gain(ks[8], (L, D_MODEL)),
        'g_post_mlp': gain(ks[9], (L, D_MODEL)),
        'w_in': nrm(ks[10], (L, D_MODEL, D_IN), D_MODEL ** -0.5),
        'a_g_q': gain(ks[11], (L, A_HEAD_DIM)),
        'a_g_k': gain(ks[12], (L, A_HEAD_DIM)),
        'b_w_alpha': nrm(ks[13], (L, 2, B_GATE_RANK, B_HEADS * B_KEY_DIM), B_GATE_RANK ** -0.5),
        'b_b_alpha': nrm(ks[14], (L, 2, B_HEADS * B_KEY_DIM), 0.1),
        'b_g_out': gain(ks[15], (L, B_VAL_DIM)),
        'c_g_q': gain(ks[16], (L, C_Q_RANK)),
        'c_g_kv': gain(ks[17], (L, C_KV_RANK)),
        'c_w_uq': nrm(ks[18], (L, C_Q_RANK, C_HEADS * (C_NOPE + C_ROPE)), C_Q_RANK ** -0.5),
        'c_w_ukv': nrm(ks[19], (L, C_KV_RANK, C_HEADS * (C_NOPE + C_VDIM)), C_KV_RANK ** -0.5),
        'w_branch': nrm(ks[20], (L, N_BRANCH, BRANCH_W, D_MODEL), BRANCH_W ** -0.5),
        'w_out': nrm(ks[21], (L, D_MODEL, D_MODEL), D_MODEL ** -0.5),
        'w_mlp_in': nrm(ks[22], (L, D_MODEL, D_FF), D_MODEL ** -0.5),
        'w_mlp_out': nrm(ks[23], (L, D_FF, D_MODEL), D_FF ** -0.5),
    }


def reference(x, c, ctx, c_ctx, w_ada, b_ada, g_pre_attn, g_post_attn, g_pre_mlp, g_post_mlp,
              w_in, a_g_q, a_g_k, b_w_alpha, b_b_alpha, b_g_out, c_g_q, c_g_kv, c_w_uq, c_w_ukv,
              w_branch, w_out, w_mlp_in, w_mlp_out):
    n_lat = x.shape[1]
    rows = n_lat // GRID_W
    rope_a = axial_rope_tables(rows, A_HEAD_DIM, x.dtype)
    rope_c = axial_rope_tables(rows, C_ROPE, x.dtype)
    xc = ctx
    for l in range(DEPTH):
        need_ctx = l < DEPTH - 1
        mod_lat = (jax.nn.silu(c) @ w_ada[l] + b_ada[l])[:, None, :]
        mod_ctx = jax.nn.silu(c_ctx) @ w_ada[l] + b_ada[l]
        sh1, sc1, gt1, sh2, sc2, gt2 = jnp.split(mod_lat, N_MOD, axis=-1)
        csh1, csc1, cgt1, csh2, csc2, cgt2 = jnp.split(mod_ctx, N_MOD, axis=-1)
        h_lat = modulate(rms_norm(x, g_pre_attn[l]), sh1, sc1)
        h_ctx = modulate(rms_norm(xc, g_pre_attn[l]), csh1, csc1)
        y_lat, y_ctx = mixing_sublayer(h_lat, h_ctx, w_in[l], a_g_q[l], a_g_k[l], b_w_alpha[l], b_b_alpha[l],
                                       b_g_out[l], c_g_q[l], c_g_kv[l], c_w_uq[l], c_w_ukv[l],
                                       w_branch[l], w_out[l], rope_a, rope_c, need_ctx)
        x = x + gt1 * rms_norm(y_lat, g_post_attn[l])
        f_lat = sq_relu_mlp(modulate(rms_norm(x, g_pre_mlp[l]), sh2, sc2), w_mlp_in[l], w_mlp_out[l])
        x = x + gt2 * rms_norm(f_lat, g_post_mlp[l])
        if need_ctx:
            xc = xc + cgt1 * rms_norm(y_ctx, g_post_attn[l])
            f_ctx = sq_relu_mlp(modulate(rms_norm(xc, g_pre_mlp[l]), csh2, csc2), w_mlp_in[l], w_mlp_out[l])
            xc = xc + cgt2 * rms_norm(f_ctx, g_post_mlp[l])
    return x
```

```python
import numpy as np
from contextlib import ExitStack
import concourse.bass as bass
import concourse.mybir as mybir
from concourse.bass_utils import run_bass_kernel_spmd

F32 = mybir.dt.float32
BF16 = mybir.dt.bfloat16
AF = mybir.ActivationFunctionType
ALU = mybir.AluOpType
AX = mybir.AxisListType

D = 1024
NT = 18
T = NT * 128
EPS = 1e-6
DIN = 6080
LN8 = float(np.log(0.125))


class Slot:
    __slots__ = ("name", "w", "r", "dsem")

    def __init__(self, name):
        self.name = name
        self.w = None
        self.r = {}
        self.dsem = None


class Buf:
    def __init__(self, t, name, psum=False):
        self.t = t
        self.s = Slot(name)
        self.psum = psum

    def __getitem__(self, k):
        return self.t[k]


class Prog:
    def __init__(self, nc, es):
        self.nc = nc
        self.es = es
        self.E = {"pe": nc.tensor, "act": nc.scalar, "dve": nc.vector, "pool": nc.gpsimd, "sp": nc.sync}
        self.sem = {k: es.enter_context(nc.semaphore("c_" + k)) for k in self.E}
        self.semh = {("c_" + k): self.sem[k] for k in self.E}
        self.cnt = {k: 0 for k in self.E}
        self.known = {k: {} for k in self.E}
        self.dval = {}
        self.nd = 0
        self.ninst = 0
        self.free_dsems = []
        self.local = []

    def _dsem(self, slot):
        if slot.dsem is None and self.free_dsems:
            slot.dsem = self.free_dsems.pop()
        if slot.dsem is None:
            name = "d%d" % self.nd
            self.nd += 1
            self.semh[name] = self.es.enter_context(self.nc.semaphore(name))
            self.dval[name] = 0
            slot.dsem = name
        return slot.dsem

    def _waits(self, eng, deps, skip_own):
        own = "c_" + eng
        need = {}
        for (s, v) in deps:
            if skip_own and s == own:
                continue
            if self.known[eng].get(s, 0) >= v:
                continue
            if need.get(s, 0) < v:
                need[s] = v
        for s, v in need.items():
            self.E[eng].wait_ge(self.semh[s], v)
            self.known[eng][s] = v
            self.ninst += 1

    def op(self, eng, fn, reads=(), writes=()):
        deps = []
        own_ = "c_" + eng
        for b in reads:
            if b.s.w is not None:
                deps.append(b.s.w)
            if b.psum:
                deps.extend(v for k, v in b.s.r.items() if k != own_)
        for b in writes:
            if b.s.w is not None:
                deps.append(b.s.w)
            deps.extend(b.s.r.values())
        self._waits(eng, deps, skip_own=(eng == "pe"))
        ins = fn(self.E[eng])
        self.cnt[eng] += 1
        ins.then_inc(self.sem[eng], 1)
        self.ninst += 1
        me = ("c_" + eng, self.cnt[eng])
        for b in reads:
            b.s.r[me[0]] = me
        for b in writes:
            b.s.w = me
            b.s.r = {}

    def dma(self, q, out, in_, reads, writes):
        dn = self._dsem(writes[0].s)
        deps = []
        for b in reads:
            if b.s.w is not None:
                deps.append(b.s.w)
        for b in writes:
            if b.s.w is not None and b.s.w[0] != dn:
                deps.append(b.s.w)
            deps.extend(v for k, v in b.s.r.items() if k != dn)
        self._waits(q, deps, skip_own=False)
        self.E[q].dma_start(out=out, in_=in_).then_inc(self.semh[dn], 16)
        self.ninst += 1
        self.dval[dn] += 16
        me = (dn, self.dval[dn])
        for b in reads:
            b.s.r[dn] = me
        for b in writes:
            b.s.w = me
            b.s.r = {}

    def barrier(self):
        cur = {("c_" + k): self.cnt[k] for k in self.E}
        cur.update(self.dval)
        for eng in self.E:
            for s, v in cur.items():
                if v > self.known[eng].get(s, 0):
                    self.E[eng].wait_ge(self.semh[s], v)
                    self.known[eng][s] = v
                    self.ninst += 1
        for b in self.local:
            if b.s.dsem is not None:
                self.free_dsems.append(b.s.dsem)
                b.s.dsem = None
                b.s.w = None
                b.s.r = {}
        self.local = []


def build(n_seq=2, n_layers=2, taps=(), stop_after=None):
    nc = bass.Bass("TRN2", target_bir_lowering=False)
    L = 2

    def din(name, shape, dt=F32):
        return nc.dram_tensor(name, list(shape), dt, kind="ExternalInput").ap()

    x_d = din("x", [2, 2048, D])
    ctx_d = din("ctx", [2, 256, D])
    cT_d = din("cT", [128, 8, 3])
    w_ada_d = din("w_ada", [L, D, 6 * D])
    badaT_d = din("badaT", [128, L, 48])
    gT_d = din("gT", [128, L, 4, 8])
    w_in_d = din("w_in", [L, D, DIN])
    gqk_d = din("gqk", [L, 128, 640])
    wal_d = din("wal", [L, 33, 512])
    gob_d = din("gob", [L, 128, 512])
    cgqT_d = din("cgqT", [128, L, 3])
    cgkvT_d = din("cgkvT", [128, L, 2])
    wuq_d = din("c_w_uq", [L, 384, 768])
    wukv_d = din("c_w_ukv", [L, 256, 1024])
    wbr_d = din("w_branch", [L, 3, 512, D])
    wout_d = din("w_out", [L, D, D])
    w1_d = din("w_mlp_in", [L, D, 4 * D])
    w2_d = din("w_mlp_out", [L, 4 * D, D])
    ident_d = din("ident", [128, 128])
    masks_d = din("masks", [128, 2, 128])
    tri_d = din("tri", [128, 4, 128])
    ropeA_d = din("ropeA", [128, 2, NT, 32])
    ropeC_d = din("ropeC", [128, 2, NT, 16])
    out_d = nc.dram_tensor("out", [2, 2048, D], F32, kind="ExternalOutput").ap()

    def dscr(name, shape, dt=BF16):
        return nc.dram_tensor(name, list(shape), dt).ap()

    w_in_b = dscr("w_in_b", [L, D, 3008])
    wuq_b = dscr("wuq_b", [L, 384, 768])
    wukv_b = dscr("wukv_b", [L, 256, 1024])
    wout_b = dscr("wout_b", [L, D, D])
    w3c_b = dscr("w3c_b", [L, 8, 128, 36 * 128])
    w1c_b = dscr("w1c_b", [L, 8, 128, 8 * 512])
    w2c_b = dscr("w2c_b", [L, 8, 128, 4 * 1024])
    oT_dd = dscr("oT_d", [128, 12, T])
    tap_out = {}

    with ExitStack() as es:
        P = Prog(nc, es)
        op, dma = P.op, P.dma

        uid = [0]

        def sb(name, shape, dt, st=es):
            uid[0] += 1
            b = Buf(st.enter_context(nc.sbuf_tensor("s%d_%s" % (uid[0], name), list(shape), dt)), name)
            if st is not es:
                P.local.append(b)
            return b

        def ps(name, shape, dt, st=es):
            uid[0] += 1
            return Buf(st.enter_context(nc.psum_tensor("p%d_%s" % (uid[0], name), list(shape), dt)), name, psum=True)

        def tap(name, src_ap, shape, reads, dt=F32):
            if name not in taps:
                return
            t = nc.dram_tensor("tap_" + name, list(shape), dt, kind="ExternalOutput").ap()
            b = Buf(None, "tap_" + name)
            if len(shape) >= 3:
                for i in range(shape[1]):
                    dma("sp", t[:, i], src_ap[:, i], reads, [b])
            else:
                dma("sp", t, src_ap, reads, [b])
            tap_out[name] = b

        wsl = {}

        def castdma(key, out, in_):
            if key not in wsl:
                wsl[key] = Buf(None, key)
            dma("pool", out, in_, [], [wsl[key]])

        for l in range(n_layers):
            for c0 in range(0, 3008, 752):
                castdma(("win", l), w_in_b[l, :, c0:c0 + 752], w_in_d[l, :, c0:c0 + 752])
            castdma(("wuq", l), wuq_b[l], wuq_d[l])
            castdma(("wukv", l), wukv_b[l], wukv_d[l])
            for fc in range(8):
                for z in range(3):
                    dg = w3c_b[l, fc, :, z * 1024:(z + 1) * 1024].rearrange("p (k j) -> p k j", k=8)
                    sg = w_in_d[l, :, 3008 + z * 1024 + fc * 128: 3008 + z * 1024 + (fc + 1) * 128].rearrange("(k p) j -> p k j", p=128)
                    castdma(("w3", l), dg, sg)
                    db = w3c_b[l, fc, :, 3072 + z * 512: 3072 + (z + 1) * 512].rearrange("p (k j) -> p k j", k=4)
                    sbr = wbr_d[l, z, :, fc * 128:(fc + 1) * 128].rearrange("(k p) j -> p k j", p=128)
                    castdma(("w3", l), db, sbr)
            castdma(("wout", l), wout_b[l], wout_d[l])
            for hc in range(8):
                castdma(("w1", l), w1c_b[l, hc].rearrange("p (k j) -> p k j", k=8),
                        w1_d[l, :, hc * 512:(hc + 1) * 512].rearrange("(k p) j -> p k j", p=128))
                castdma(("w2", l), w2c_b[l, hc].rearrange("p (k j) -> p k j", k=4),
                        w2_d[l, hc * 512:(hc + 1) * 512, :].rearrange("(k p) j -> p k j", p=128))

        xs = sb("xs", [128, NT, D], F32)
        xs_s = [Buf(xs.t, "xs%d" % t) for t in range(NT)]
        hT = sb("hT", [128, 8, T], BF16)
        hT_s = [Buf(hT.t, "hT%d" % t) for t in range(NT)]
        identf = sb("identf", [128, 128], F32)
        identb = sb("identb", [128, 128], BF16)
        cT = sb("cT", [128, 8, 3], F32)
        siluT = sb("siluT", [128, 8, 3], F32)
        badaT = sb("badaT", [128, L, 48], F32)
        gT = sb("gT", [128, L, 4, 8], F32)
        dv = sb("dv", [128, L, 6, 8, 3], F32)
        cgqT = sb("cgqT", [128, L, 3], F32)
        cgkvT = sb("cgkvT", [128, L, 2], F32)
        cneg = sb("cneg", [128, 2], F32)
        stat = sb("stat", [128, NT, 4], F32)
        stat_s = [Buf(stat.t, "stat%d" % t) for t in range(NT)]
        epsb = sb("epsb", [128, 1], F32)

        for (b, d) in ((identf, ident_d),
                       (cT, cT_d), (badaT, badaT_d), (gT, gT_d), (cgqT, cgqT_d), (cgkvT, cgkvT_d)):
            dma("sp", b[:], d, [], [b])
        op("dve", lambda e: e.tensor_copy(out=identb[:], in_=identf[:]), [identf], [identb])
        op("dve", lambda e: e.memset(cneg[:], -1.0 / 16.0), [], [cneg])
        op("dve", lambda e: e.memset(epsb[:], EPS), [], [epsb])

        with ExitStack() as st:
            wab = [sb("wab%d" % i, [128, 8, 512], F32, st) for i in range(2)]
            mps = [ps("mps%d" % i, [128, 128, 4], F32, st) for i in range(2)]
            tmp = sb("modtmp", [128, 8, 3], F32, st)
            modT = sb("modT", [128, L, 48, 3], F32, st)
            op("act", lambda e: e.activation(out=tmp[:], in_=cT[:], func=AF.Exp, scale=-1.0), [cT], [tmp])
            op("dve", lambda e: e.tensor_scalar(out=tmp[:], in0=tmp[:], scalar1=1.0, scalar2=None, op0=ALU.add), [tmp], [tmp])
            op("dve", lambda e: e.reciprocal(out=tmp[:], in_=tmp[:]), [tmp], [tmp])
            op("dve", lambda e: e.tensor_tensor(out=siluT[:], in0=tmp[:], in1=cT[:], op=ALU.mult), [tmp, cT], [siluT])
            i = 0
            for l in range(n_layers):
                for cc in range(12):
                    w = wab[i % 2]
                    m = mps[i % 2]
                    i += 1
                    dma("sp", w[:], w_ada_d[l, :, cc * 512:(cc + 1) * 512].rearrange("(k p) n -> p k n", p=128), [], [w])
                    for mc in range(4):
                        for kc in range(8):
                            op("pe", lambda e, w=w, m=m, mc=mc, kc=kc: e.matmul(
                                m[:, mc, 0:3], lhsT=w[:, kc, mc * 128:(mc + 1) * 128], rhs=siluT[:, kc, :],
                                start=(kc == 0), stop=(kc == 7)), [w, siluT], [m])
                    op("dve", lambda e, m=m, l=l, cc=cc: e.tensor_tensor(
                        out=modT[:, l, cc * 4:(cc + 1) * 4, :], in0=m[:, 0:4, 0:3],
                        in1=badaT[:, l, cc * 4:(cc + 1) * 4].unsqueeze(2).broadcast_to([128, 4, 3]), op=ALU.add),
                       [m, badaT], [modT])
            for l in range(n_layers):
                def gb(kind, l=l):
                    return gT[:, l, kind, :].unsqueeze(2).broadcast_to([128, 8, 3])
                for (dst, scc, gk) in ((0, 1, 0), (3, 4, 2)):
                    op("dve", lambda e, dst=dst, scc=scc, l=l: e.tensor_scalar(
                        out=dv[:, l, dst], in0=modT[:, l, scc * 8:(scc + 1) * 8, :], scalar1=1.0, scalar2=None, op0=ALU.add),
                       [modT], [dv])
                    op("dve", lambda e, dst=dst, gk=gk, l=l: e.tensor_tensor(
                        out=dv[:, l, dst], in0=dv[:, l, dst], in1=gb(gk), op=ALU.mult), [dv, gT], [dv])
                for (dst, shc) in ((1, 0), (4, 3)):
                    op("dve", lambda e, dst=dst, shc=shc, l=l: e.tensor_copy(
                        out=dv[:, l, dst], in_=modT[:, l, shc * 8:(shc + 1) * 8, :]), [modT], [dv])
                for (dst, gtc, gk) in ((2, 2, 1), (5, 5, 3)):
                    op("dve", lambda e, dst=dst, gtc=gtc, gk=gk, l=l: e.tensor_tensor(
                        out=dv[:, l, dst], in0=modT[:, l, gtc * 8:(gtc + 1) * 8, :], in1=gb(gk), op=ALU.mult),
                       [modT, gT], [dv])
            tap("modT", modT[:], [128, L, 48, 3], [modT])
            tap("dv", dv[:], [128, L, 6, 8, 3], [dv])
            P.barrier()

        if stop_after == "mod":
            n_seq = 0

        def tiles_for(l):
            return list(range(NT))

        def out_tiles(l):
            return list(range(NT)) if l < L - 1 else list(range(2, NT))

        def rstd_from_ss(ss_ap, out_ap, n, reads_writes):
            op("act", lambda e: e.activation(out=out_ap, in_=ss_ap, func=AF.Ln, scale=1.0 / n, bias=epsb[0:ss_ap.shape[0], 0:1]),
               reads_writes + [epsb], reads_writes)
            op("act", lambda e: e.activation(out=out_ap, in_=out_ap, func=AF.Exp, scale=-0.5), reads_writes, reads_writes)

        def phase_norm(l, s, which, st):
            junk = sb("junk", [128, D], BF16, st)
            xn = [sb("xn%d" % i, [128, D], BF16, st) for i in range(2)]
            pT = [ps("pT%d" % i, [128, D], BF16, st) for i in range(2)]
            gsi, shi = (0, 1) if which == 0 else (3, 4)
            for t in range(NT):
                j = 2 if t < 2 else s
                sst = stat_s[t]
                op("act", lambda e, t=t: e.activation(out=junk[:], in_=xs[:, t, :], func=AF.Square, accum_out=stat[:, t, 0:1]),
                   [xs_s[t]], [junk, sst])
                rstd_from_ss(stat[:, t, 0:1], stat[:, t, 1:2], D, [sst])
                xb = xn[t % 2]
                pb = pT[t % 2]
                op("act", lambda e, t=t, xb=xb: e.activation(out=xb[:], in_=xs[:, t, :], func=AF.Copy, scale=stat[:, t, 1:2]),
                   [xs_s[t], sst], [xb])
                for kc in range(8):
                    op("pe", lambda e, kc=kc, xb=xb, pb=pb: e.transpose(pb[:, kc * 128:(kc + 1) * 128], xb[:, kc * 128:(kc + 1) * 128], identb[:]),
                       [xb, identb], [pb])
                for kc in range(8):
                    op("dve", lambda e, kc=kc, t=t, j=j, pb=pb: e.tensor_scalar(
                        out=hT[:, kc, t * 128:(t + 1) * 128], in0=pb[:, kc * 128:(kc + 1) * 128],
                        scalar1=dv[:, l, gsi, kc, j:j + 1], scalar2=dv[:, l, shi, kc, j:j + 1], op0=ALU.mult, op1=ALU.add),
                       [pb, dv], [hT_s[t]])

        def load_w(buf, src, key):
            dma("sp", buf[:], src, [wsl[key]], [buf])

        def proj_tok(psb, t, wbuf, c0, c1, extra_reads=()):
            n = c1 - c0
            for kc in range(8):
                op("pe", lambda e, kc=kc: e.matmul(psb[:, 0:n], lhsT=hT[:, kc, t * 128:(t + 1) * 128], rhs=wbuf[:, kc, c0:c1],
                                                  start=(kc == 0), stop=(kc == 7)), [hT_s[t], wbuf], [psb])

        def attention(st_bufs, pieces, mtiles, scale, lhsT_fn, k_reads, V_fn, v_reads, q_reads, nblk, dest_fn, dest_buf):
            Sp, Pt, Ob, rden = st_bufs
            nm = len(mtiles)
            ncols = nblk * 128
            for mi, mt in enumerate(mtiles):
                sp_ = Sp[mi % 2]
                pt_ = Pt[mi % len(Pt)]
                for (pr, rhs_ap, n, c0) in pieces:
                    op("pe", lambda e, pr=pr, rhs_ap=rhs_ap, n=n, c0=c0, mt=mt, sp_=sp_: e.matmul(
                        sp_[:, c0:c0 + n], lhsT=lhsT_fn(pr, mt), rhs=rhs_ap, start=True, stop=True),
                       list(k_reads) + list(q_reads), [sp_])
                op("act", lambda e, sp_=sp_, pt_=pt_: e.activation(out=pt_[:, 0:ncols], in_=sp_[:, 0:ncols], func=AF.Exp, scale=scale),
                   [sp_], [pt_])
                for b in range(nblk):
                    op("pe", lambda e, b=b, mt=mt, mi=mi, pt_=pt_: e.matmul(
                        Ob[b][:, 0:65], lhsT=pt_[:, b * 128:(b + 1) * 128], rhs=V_fn(mt), start=(mi == 0), stop=(mi == nm - 1)),
                       [pt_] + list(v_reads), [Ob[b]])
            for b in range(nblk):
                op("dve", lambda e, b=b: e.reciprocal(out=rden[:, b:b + 1], in_=Ob[b][:, 64:65]), [Ob[b]], [rden])
                op("act", lambda e, b=b: e.activation(out=dest_fn(b), in_=Ob[b][:, 0:64], func=AF.Copy, scale=rden[:, b:b + 1]),
                   [Ob[b], rden], [dest_buf])

        def store_oT(o_tok, z, t, oTs, pTo):
            for c in range(4):
                op("pe", lambda e, c=c: e.transpose(pTo[:, c * 128:(c + 1) * 128], o_tok[:, c * 128:(c + 1) * 128], identb[:]),
                   [o_tok, identb], [pTo])
            op("act", lambda e: e.activation(out=oTs[:].rearrange("p a b -> p (a b)"), in_=pTo[:, 0:512], func=AF.Copy), [pTo], [oTs])
            dma("pool", oT_dd[:, 4 * z:4 * z + 4, t * 128:(t + 1) * 128], oTs[:], [oTs], [oT_slot])

        oT_slot = Buf(None, "oT_dram")

        for s in range(n_seq):
            for t in range(NT):
                src = ctx_d[s, t * 128:(t + 1) * 128, :] if t < 2 else x_d[s, (t - 2) * 128:(t - 1) * 128, :]
                dma("sp", xs[:, t, :], src, [], [xs_s[t]])
            for l in range(n_layers):
                last = (l == L - 1)
                qtiles = list(range(2, NT)) if last else list(range(NT))

                with ExitStack() as st:
                    phase_norm(l, s, 0, st)
                    if s == 0 and l == 0:
                        tap("hT", hT[:], [128, 8, T], hT_s, BF16)
                    P.barrier()
                if stop_after == "n1":
                    break

                with ExitStack() as st:
                    wA = sb("wA", [128, 8, 768], BF16, st)
                    load_w(wA, w_in_b[l, :, 0:768].rearrange("(k p) n -> p k n", p=128), ("win", l))
                    gqk = sb("gqk", [128, 640], F32, st)
                    ropeA = sb("ropeA", [128, 2, NT, 32], F32, st)
                    dma("sp", ropeA[:], ropeA_d, [], [ropeA])
                    dma("sp", gqk[:], gqk_d[l], [], [gqk])
                    kT = sb("kT", [64, 2, T], BF16, st)
                    VA = sb("VA", [128, NT, 2, 65], BF16, st)
                    op("dve", lambda e: e.memset(VA[:].rearrange("p a b c -> p (a b c)"), 1.0), [], [VA])
                    zp = [ps("zpA%d" % i, [128, 512], F32, st) for i in range(2)]
                    pTb = ps("pTbA", [128, 1024], BF16, st)
                    Sp = [zp[0], zp[1]]
                    Ob = [ps("ObA%d" % i, [128, 512], F32, st) for i in range(4)]
                    Pt = [sb("PtA%d" % i, [128, 512], BF16, st) for i in range(3)]
                    rden = sb("rdenA", [128, 4], F32, st)
                    sq = sb("sqA", [128, 640], F32, st)
                    qn = sb("qnA", [128, 640], F32, st)
                    tmpa = sb("tmpaA", [128, 320], F32, st)
                    tmpb = sb("tmpbA", [128, 320], F32, st)
                    qr = sb("qrA", [128, 640], BF16, st)
                    qT = sb("qTA", [64, 8, 128], BF16, st)
                    ssq = sb("ssqA", [128, 10], F32, st)
                    oA = sb("oA", [128, 512], BF16, st)
                    oTs = sb("oTsA", [128, 4, 128], BF16, st)

                    def qk_norm_rope(psb, c0, nh, t, goff):
                        w_ = nh * 64
                        op("act", lambda e: e.activation(out=sq[:, 0:w_], in_=psb[:, c0:c0 + w_], func=AF.Square), [psb], [sq])
                        op("dve", lambda e: e.tensor_reduce(out=ssq[:, 0:nh], in_=sq[:, 0:w_].rearrange("p (h d) -> p h d", h=nh),
                                                            axis=AX.X, op=ALU.add), [sq], [ssq])
                        rstd_from_ss(ssq[:, 0:nh], ssq[:, 0:nh], 64, [ssq])
                        op("dve", lambda e: e.tensor_tensor(
                            out=qn[:, 0:w_].rearrange("p (h d) -> p h d", h=nh), in0=psb[:, c0:c0 + w_].rearrange("p (h d) -> p h d", h=nh),
                            in1=ssq[:, 0:nh].unsqueeze(2).broadcast_to([128, nh, 64]), op=ALU.mult), [psb, ssq], [qn])
                        op("pool", lambda e: e.tensor_tensor(out=qn[:, 0:w_], in0=qn[:, 0:w_], in1=gqk[:, goff:goff + w_], op=ALU.mult),
                           [qn, gqk], [qn])

                    def rope_to(src, nh, t, dst_fn, dst_buf, rope, half):
                        v = src[:, 0:nh * 2 * half].rearrange("p (h i two) -> p h i two", h=nh, two=2)
                        x0 = v[:, :, :, 0]
                        x1 = v[:, :, :, 1]
                        cs = rope[:, 0, t, :].unsqueeze(1).broadcast_to([128, nh, half])
                        sn = rope[:, 1, t, :].unsqueeze(1).broadcast_to([128, nh, half])
                        ta = tmpa[:, 0:nh * half].rearrange("p (h i) -> p h i", h=nh)
                        tb = tmpb[:, 0:nh * half].rearrange("p (h i) -> p h i", h=nh)
                        op("dve", lambda e: e.tensor_tensor(out=ta, in0=x0, in1=cs, op=ALU.mult), [src, rope], [tmpa])
                        op("dve", lambda e: e.tensor_tensor(out=tb, in0=x1, in1=sn, op=ALU.mult), [src, rope], [tmpb])
                        op("dve", lambda e: e.tensor_tensor(out=dst_fn(0), in0=ta, in1=tb, op=ALU.subtract), [tmpa, tmpb], [dst_buf])
                        op("dve", lambda e: e.tensor_tensor(out=ta, in0=x0, in1=sn, op=ALU.mult), [src, rope], [tmpa])
                        op("dve", lambda e: e.tensor_tensor(out=tb, in0=x1, in1=cs, op=ALU.mult), [src, rope], [tmpb])
                        op("dve", lambda e: e.tensor_tensor(out=dst_fn(1), in0=ta, in1=tb, op=ALU.add), [tmpa, tmpb], [dst_buf])

                    for t in range(NT):
                        z = zp[t % 2]
                        proj_tok(z, t, wA, 512, 768)
                        qk_norm_rope(z, 0, 2, t, 512)
                        kvv = qr[:, 512:640].rearrange("p (kv d) -> p kv d", kv=2)
                        rope_to(qn, 2, t, lambda wh: kvv[:, :, wh * 32:(wh + 1) * 32], qr, ropeA, 32)
                        op("act", lambda e, z=z, t=t: e.activation(out=VA[:, t, :, 0:64], in_=z[:, 128:256].rearrange("p (kv d) -> p kv d", kv=2), func=AF.Copy),
                           [z], [VA])
                        for kv in range(2):
                            op("pe", lambda e, kv=kv: e.transpose(pTb[0:64, kv * 128:(kv + 1) * 128], qr[:, 512 + kv * 64:512 + (kv + 1) * 64], identb[:]),
                               [qr, identb], [pTb])
                        op("dve", lambda e, t=t: e.tensor_copy(out=kT[:, :, t * 128:(t + 1) * 128], in_=pTb[0:64, 0:256].rearrange("p (kv n) -> p kv n", kv=2)),
                           [pTb], [kT])
                    if s == 0 and l == 0:
                        pass
                        pass
                    for t in qtiles:
                        z = Ob[0]
                        proj_tok(z, t, wA, 0, 512)
                        qk_norm_rope(z, 0, 8, t, 0)
                        qv = qr[:, 0:512].rearrange("p (h d) -> p h d", h=8)
                        rope_to(qn, 8, t, lambda wh: qv[:, :, wh * 32:(wh + 1) * 32], qr, ropeA, 32)
                        for h in range(8):
                            op("pe", lambda e, h=h: e.transpose(pTb[0:64, h * 128:(h + 1) * 128], qr[:, h * 64:(h + 1) * 64], identb[:]),
                               [qr, identb], [pTb])
                        op("dve", lambda e: e.tensor_copy(out=qT[:].rearrange("p a b -> p (a b)"), in_=pTb[0:64, 0:1024]), [pTb], [qT])
                        mtiles = [0, 1] if t < 2 else list(range(NT))
                        for kv in range(2):
                            pieces = [(0, qT[0:64, 4 * kv:4 * kv + 4, :], 512, 0)]
                            heads = [4 * kv, 4 * kv + 1, 4 * kv + 2, 4 * kv + 3]
                            attention((Sp, Pt, Ob, rden), pieces, mtiles, 0.125,
                                      lambda pr, mt, kv=kv: kT[pr:pr + 64, kv, mt * 128:(mt + 1) * 128], [kT],
                                      lambda mt, kv=kv: VA[:, mt, kv, :], [VA], [qT], 4,
                                      lambda b, heads=heads: oA[:, heads[b] * 64:(heads[b] + 1) * 64], oA)
                        if s == 0 and l == 0 and t == 2:
                            tap("oA2", oA[:], [128, 512], [oA], BF16)
                        store_oT(oA, 0, t, oTs, pTb)
                    P.barrier()
                if stop_after == "A":
                    break

                with ExitStack() as st:
                    wC = sb("wC", [128, 8, 672], BF16, st)
                    load_w(wC, w_in_b[l, :, 2336:3008].rearrange("(k p) n -> p k n", p=128), ("win", l))
                    wuq = sb("wuq", [128, 3, 768], BF16, st)
                    load_w(wuq, wuq_b[l].rearrange("(k p) n -> p k n", p=128), ("wuq", l))
                    wukv = sb("wukv", [128, 2, 1024], BF16, st)
                    load_w(wukv, wukv_b[l].rearrange("(k p) n -> p k n", p=128), ("wukv", l))
                    kTC = sb("kTC", [128, 8, T], BF16, st)
                    ropeC = sb("ropeC", [128, 2, NT, 16], F32, st)
                    dma("sp", ropeC[:], ropeC_d, [], [ropeC])
                    VC = sb("VC", [128, NT, 8, 65], BF16, st)
                    op("dve", lambda e: e.memset(VC[:].rearrange("p a b c -> p (a b c)"), 1.0), [], [VC])
                    zp = [ps("zpC%d" % i, [128, 512], F32, st) for i in range(2)]
                    pTb = ps("pTbC", [128, 1024], BF16, st)
                    Ob = [ps("ObC%d" % i, [128, 512], F32, st) for i in range(4)]
                    kvp = Ob[1]
                    Sp = [zp[0], zp[1]]
                    Pt = [sb("PtC%d" % i, [128, 512], BF16, st) for i in range(2)]
                    rden = sb("rdenC", [128, 4], F32, st)
                    cn = sb("cnC", [128, 384], BF16, st)
                    sqc = Buf(Pt[0].t, "sqc")
                    sqc.s = Pt[0].s
                    cnT = sb("cnTC", [128, 3, 128], BF16, st)
                    Kf = sb("KfC", [128, 8, 128], BF16, st)
                    op("dve", lambda e: e.memset(Kf[:].rearrange("p a b -> p (a b)"), 0.0), [], [Kf])
                    krr = sb("krrC", [128, 32], BF16, st)
                    krf = sb("krfC", [128, 32], F32, st)
                    tmpa = sb("tmpaC", [128, 64], F32, st)
                    tmpb = sb("tmpbC", [128, 64], F32, st)
                    qTC = sb("qTCg", [128, 8, 512], BF16, st)
                    oC = [sb("oC%d" % i, [128, 512], BF16, st) for i in range(4)]
                    oTs = sb("oTsC", [128, 4, 128], BF16, st)
                    st4 = sb("st4C", [128, 4], F32, st)

                    def rope_c(src_ap_fn, nh, t, dst_fn, srcs, dst_buf):
                        cs = ropeC[:, 0, t, :].unsqueeze(1).broadcast_to([128, nh, 16])
                        sn = ropeC[:, 1, t, :].unsqueeze(1).broadcast_to([128, nh, 16])
                        ta = tmpa[:, 0:nh * 16].rearrange("p (h i) -> p h i", h=nh)
                        tb = tmpb[:, 0:nh * 16].rearrange("p (h i) -> p h i", h=nh)
                        x0, x1 = src_ap_fn(0), src_ap_fn(1)
                        op("dve", lambda e: e.tensor_tensor(out=ta, in0=x0, in1=cs, op=ALU.mult), srcs + [ropeC], [tmpa])
                        op("dve", lambda e: e.tensor_tensor(out=tb, in0=x1, in1=sn, op=ALU.mult), srcs + [ropeC], [tmpb])
                        op("dve", lambda e: e.tensor_tensor(out=dst_fn(0), in0=ta, in1=tb, op=ALU.subtract), [tmpa, tmpb], [dst_buf])
                        op("dve", lambda e: e.tensor_tensor(out=ta, in0=x0, in1=sn, op=ALU.mult), srcs + [ropeC], [tmpa])
                        op("dve", lambda e: e.tensor_tensor(out=tb, in0=x1, in1=cs, op=ALU.mult), srcs + [ropeC], [tmpb])
                        op("dve", lambda e: e.tensor_tensor(out=dst_fn(1), in0=ta, in1=tb, op=ALU.add), [tmpa, tmpb], [dst_buf])

                    def norm_T(psb, c0, n, gvec, t):
                        nk = n // 128
                        op("act", lambda e: e.activation(out=sqc[:, 0:n], in_=psb[:, c0:c0 + n], func=AF.Square), [psb], [sqc])
                        op("dve", lambda e: e.tensor_reduce(out=st4[:, 0:1], in_=sqc[:, 0:n], axis=AX.X, op=ALU.add), [sqc], [st4])
                        rstd_from_ss(st4[:, 0:1], st4[:, 1:2], n, [st4])
                        op("act", lambda e: e.activation(out=cn[:, 0:n], in_=psb[:, c0:c0 + n], func=AF.Copy, scale=st4[:, 1:2]), [psb, st4], [cn])
                        for kc in range(nk):
                            op("pe", lambda e, kc=kc: e.transpose(pTb[:, kc * 128:(kc + 1) * 128], cn[:, kc * 128:(kc + 1) * 128], identb[:]),
                               [cn, identb], [pTb])
                        for kc in range(nk):
                            op("dve", lambda e, kc=kc: e.tensor_scalar(out=cnT[:, kc, :], in0=pTb[:, kc * 128:(kc + 1) * 128],
                                                                       scalar1=gvec[:, l, kc:kc + 1], scalar2=None, op0=ALU.mult),
                               [pTb, gvec], [cnT])

                    import os as _os
                    for t in range(NT):
                        if "cpre" in _os.environ.get("KSKIP", ""):
                            continue
                        z = zp[t % 2]
                        proj_tok(z, t, wC, 384, 672)
                        norm_T(z, 0, 256, cgkvT, t)
                        for half in range(2):
                            pb = kvp if half == 0 else Ob[0]
                            for kc in range(2):
                                op("pe", lambda e, kc=kc, half=half, pb=pb: e.matmul(pb[:, 0:512], lhsT=cnT[:, kc, :], rhs=wukv[:, kc, half * 512:(half + 1) * 512],
                                                                           start=(kc == 0), stop=(kc == 1)), [cnT, wukv], [pb])
                            pv = pb[:, 0:512].rearrange("p (h c) -> p h c", h=4)
                            op("act", lambda e, pv=pv, half=half: e.activation(out=Kf[:, half * 4:(half + 1) * 4, 0:64], in_=pv[:, :, 0:64], func=AF.Copy), [pb], [Kf])
                            op("act", lambda e, pv=pv, half=half, t=t: e.activation(out=VC[:, t, half * 4:(half + 1) * 4, 0:64], in_=pv[:, :, 64:128], func=AF.Copy), [pb], [VC])
                        op("act", lambda e, z=z: e.activation(out=krf[:], in_=z[:, 256:288], func=AF.Copy), [z], [krf])
                        krv = krf[:].rearrange("p (o i two) -> p o i two", o=1, two=2)
                        if "crope" in _os.environ.get("KSKIP", ""):
                            op("dve", lambda e: e.tensor_copy(out=krr[:], in_=krf[:]), [krf], [krr])
                        else:
                            rope_c(lambda two: krv[:, :, :, two], 1, t,
                                   lambda wh: krr[:, wh * 16:(wh + 1) * 16].unsqueeze(1), [krf], krr)
                        op("dve", lambda e: e.tensor_copy(out=Kf[:, :, 64:96], in_=krr[:].unsqueeze(1).broadcast_to([128, 8, 32])), [krr], [Kf])
                        for h in range(8):
                            op("pe", lambda e, h=h: e.transpose(pTb[:, h * 128:(h + 1) * 128], Kf[:, h, :], identb[:]), [Kf, identb], [pTb])
                        op("dve", lambda e, t=t: e.tensor_copy(out=kTC[:, :, t * 128:(t + 1) * 128], in_=pTb[:, :].rearrange("p (h n) -> p h n", h=8)),
                           [pTb], [kTC])
                    if s == 0 and l == 0:
                        pass
                        pass
                    groups = ([] if last else [[0, 1]]) + [list(range(2 + 4 * g, 6 + 4 * g)) for g in range(4)]
                    import os as _os
                    if "cattn" in _os.environ.get("KSKIP", "").split(","):
                        groups = []
                    for grp in groups:
                        for gi, t in enumerate(grp):
                            z = zp[gi % 2]
                            proj_tok(z, t, wC, 0, 384)
                            norm_T(z, 0, 384, cgqT, t)
                            for half in range(2):
                                pb = kvp if half == 0 else Ob[0]
                                for kc in range(3):
                                    op("pe", lambda e, kc=kc, half=half, pb=pb: e.matmul(pb[:, 0:384], lhsT=cnT[:, kc, :], rhs=wuq[:, kc, half * 384:(half + 1) * 384],
                                                                               start=(kc == 0), stop=(kc == 2)), [cnT, wuq], [pb])
                                pv = pb[:, 0:384].rearrange("p (h c) -> p h c", h=4)
                                op("act", lambda e, pv=pv, half=half: e.activation(out=Kf[:, half * 4:(half + 1) * 4, 0:64], in_=pv[:, :, 0:64], func=AF.Copy), [pb], [Kf])
                                prr = pv[:, :, 64:96].rearrange("p h (i two) -> p h i two", two=2)
                                if "cqrope" not in _os.environ.get("KSKIP", "").split(","):
                                    rope_c(lambda two, prr=prr: prr[:, :, :, two], 4, t,
                                           lambda wh, half=half: Kf[:, half * 4:(half + 1) * 4, 64 + wh * 16:64 + (wh + 1) * 16], [pb], Kf)
                            for h in range(8):
                                op("pe", lambda e, h=h: e.transpose(pTb[:, h * 128:(h + 1) * 128], Kf[:, h, :], identb[:]), [Kf, identb], [pTb])
                            op("dve", lambda e, gi=gi: e.tensor_copy(out=qTC[:, :, gi * 128:(gi + 1) * 128], in_=pTb[:, :].rearrange("p (h n) -> p h n", h=8)),
                               [pTb], [qTC])
                        nb = len(grp)
                        mtiles = [0, 1] if grp[0] < 2 else list(range(NT))
                        for h in range(8):
                            if "catt" in _os.environ.get("KSKIP", "").split(","):
                                continue
                            pieces = [(0, qTC[:, h, 0:nb * 128], nb * 128, 0)]
                            attention((Sp, Pt, Ob, rden), pieces, mtiles, float(96 ** -0.5),
                                      lambda pr, mt, h=h: kTC[:, h, mt * 128:(mt + 1) * 128], [kTC],
                                      lambda mt, h=h: VC[:, mt, h, :], [VC], [qTC], nb,
                                      lambda b, h=h: oC[b][:, h * 64:(h + 1) * 64], oC[0])
                        for gi, t in enumerate(grp):
                            if s == 0 and l == 0 and t == 2:
                                tap("oC2", oC[gi][:], [128, 512], [oC[0]], BF16)
                            tmpbuf = Buf(oC[gi].t, "x")
                            tmpbuf.s = oC[0].s
                            if "cstore" not in _os.environ.get("KSKIP", "").split(","):
                                store_oT(tmpbuf, 2, t, oTs, pTb)
                    P.barrier()
                if stop_after == "C":
                    break

                with ExitStack() as st:
                    wB = sb("wB", [128, 8, 1568], BF16, st)
                    load_w(wB, w_in_b[l, :, 768:2336].rearrange("(k p) n -> p k n", p=128), ("win", l))
                    walf = sb("walf", [33, 512], F32, st)
                    dma("sp", walf[:], wal_d[l], [], [walf])
                    walb = sb("walb", [33, 512], BF16, st)
                    op("dve", lambda e: e.tensor_copy(out=walb[:], in_=walf[:]), [walf], [walb])
                    gob = sb("gob", [128, 512], F32, st)
                    masks = sb("masks", [128, 2, 128], F32, st)
                    tri = sb("tri", [128, 4, 128], F32, st)
                    dma("sp", masks[:], masks_d, [], [masks])
                    dma("sp", tri[:], tri_d, [], [tri])
                    dma("sp", gob[:], gob_d[l], [], [gob])
                    bk = [ps("bkB%d" % i, [128, 512], F32, st) for i in range(7)]
                    pTb = ps("pTbB", [128, 1024], BF16, st)
                    Sst = sb("Sst", [128, 4, 128], F32, st)
                    Sall = sb("Sall", [128, NT, 4, 128], BF16, st)
                    alrT = sb("alrT", [33, 128], BF16, st)
                    spb = sb("spb", [128, 512], F32, st)
                    E1 = sb("E1", [128, 512], F32, st)
                    E2 = sb("E2", [128, 512], F32, st)
                    E3 = sb("E3", [128, 512], F32, st)
                    qd = sb("qd", [128, 512], BF16, st)
                    kd = sb("kd", [128, 512], BF16, st)
                    kte = sb("kte", [128, 512], BF16, st)
                    Vb = sb("Vb", [128, 512], BF16, st)
                    gS = sb("gS", [128, 4], F32, st)
                    qkT = sb("qkT", [128, 8, 128], BF16, st)
                    Am = sb("Am", [128, 2, 4, 128], BF16, st)
                    sqo = sb("sqo", [128, 512], F32, st)
                    so4 = sb("so4", [128, 4], F32, st)
                    eg = sb("egB", [128, 512], F32, st)
                    sg = sb("sgB", [128, 512], F32, st)
                    on = sb("onB", [128, 512], F32, st)
                    oB = sb("oB", [128, 512], BF16, st)
                    oTs = sb("oTsB", [128, 4, 128], BF16, st)
                    op("dve", lambda e: e.memset(Sst[:], 0.0), [], [Sst])
                    op("dve", lambda e: e.memset(alrT[32:33, :], 1.0), [], [alrT])

                    def gla_prep(t, full):
                        zqk, zv, lg, c1, c2, misc, up = bk[0], bk[1], bk[2], bk[3], bk[4], bk[5], bk[6]
                        proj_tok(zqk, t, wB, 0, 512)
                        proj_tok(zv, t, wB, 512, 1024)
                        for kc in range(8):
                            op("pe", lambda e, kc=kc: e.matmul(misc[0:32, 0:128], lhsT=wB[:, kc, 1024:1056], rhs=hT[:, kc, t * 128:(t + 1) * 128],
                                                              start=(kc == 0), stop=(kc == 7)), [wB, hT_s[t]], [misc])
                        op("act", lambda e: e.activation(out=alrT[0:32, :], in_=misc[0:32, 0:128], func=AF.Copy), [misc], [alrT])
                        op("act", lambda e: e.activation(out=Vb[:], in_=zv[:, 0:512], func=AF.Copy), [zv], [Vb])
                        op("pe", lambda e: e.matmul(lg[:, 0:512], lhsT=alrT[:, :], rhs=walb[:, :], start=True, stop=True), [alrT, walb], [lg])
                        op("act", lambda e: e.activation(out=spb[:], in_=lg[:, 0:512], func=AF.Exp, scale=-1.0), [lg], [spb])
                        op("act", lambda e: e.activation(out=spb[:], in_=spb[:], func=AF.Ln, bias=1.0), [spb], [spb])
                        op("pe", lambda e: e.matmul(c1[:, 0:256], lhsT=tri[:, 0, :], rhs=spb[:, 0:256], start=True, stop=True), [tri, spb], [c1])
                        op("pe", lambda e: e.matmul(c1[:, 256:512], lhsT=tri[:, 2, :], rhs=spb[:, 256:512], start=True, stop=True), [tri, spb], [c1])
                        op("pe", lambda e: e.matmul(c2[:, 0:256], lhsT=tri[:, 1, :], rhs=spb[:, 0:256], start=True, stop=True), [tri, spb], [c2])
                        op("pe", lambda e: e.matmul(c2[:, 256:512], lhsT=tri[:, 3, :], rhs=spb[:, 256:512], start=True, stop=True), [tri, spb], [c2])
                        for h in range(4):
                            for r in range(2):
                                op("pe", lambda e, h=h, r=r: e.matmul(misc[64 * r:64 * r + 64, 128 + h:129 + h], lhsT=spb[:, r * 256 + h * 64:r * 256 + (h + 1) * 64],
                                                                      rhs=cneg[:, 0:1], start=True, stop=True), [spb, cneg], [misc])
                        op("act", lambda e: e.activation(out=gS[:], in_=misc[:, 128:132], func=AF.Exp), [misc], [gS])
                        op("act", lambda e: e.activation(out=E3[:], in_=c2[:, 0:512], func=AF.Exp), [c2], [E3])
                        kb = zqk[:, 256:512].rearrange("p (h d) -> p h d", h=4).unsqueeze(1).broadcast_to([128, 2, 4, 64])
                        qb = zqk[:, 0:256].rearrange("p (h d) -> p h d", h=4).unsqueeze(1).broadcast_to([128, 2, 4, 64])

                        def ov(b):
                            return b[:].rearrange("p (h r d) -> p r h d", h=4, r=2)

                        def iv(b):
                            return b[:].rearrange("p (r h d) -> p r h d", r=2, h=4)
                        op("dve", lambda e: e.tensor_tensor(out=ov(kte), in0=kb, in1=iv(E3), op=ALU.mult), [zqk, E3], [kte])
                        if full:
                            op("act", lambda e: e.activation(out=E1[:], in_=c1[:, 0:512], func=AF.Exp, bias=LN8), [c1], [E1])
                            op("act", lambda e: e.activation(out=E2[:], in_=c1[:, 0:512], func=AF.Exp, scale=-1.0), [c1], [E2])
                            op("dve", lambda e: e.tensor_tensor(out=ov(qd), in0=qb, in1=iv(E1), op=ALU.mult), [zqk, E1], [qd])
                            op("dve", lambda e: e.tensor_tensor(out=ov(kd), in0=kb, in1=iv(E2), op=ALU.mult), [zqk, E2], [kd])
                        ktv = kte[:].rearrange("p (h r d) -> p h r d", h=4, r=2)
                        r = 0 if full else 1
                        for h in range(4):
                            op("pe", lambda e, h=h, r=r: e.matmul(up[64 * r:64 * r + 64, h * 128:(h + 1) * 128], lhsT=ktv[:, h, r, :], rhs=Vb[:, h * 128:(h + 1) * 128],
                                                                start=True, stop=True), [kte, Vb], [up])

                    def state_update(r):
                        up = bk[6]
                        for h in range(4):
                            op("dve", lambda e, h=h: e.scalar_tensor_tensor(
                                out=Sst[64 * r:64 * r + 64, h, :], in0=Sst[64 * r:64 * r + 64, h, :], scalar=gS[64 * r:64 * r + 64, h:h + 1],
                                in1=up[64 * r:64 * r + 64, h * 128:(h + 1) * 128], op0=ALU.mult, op1=ALU.add), [Sst, gS, up], [Sst])

                    for t in [1, 0] + list(range(NT - 1, 1, -1)):
                        gla_prep(t, False)
                        op("act", lambda e, t=t: e.activation(out=Sall[64:128, t], in_=Sst[64:128], func=AF.Copy), [Sst], [Sall])
                        state_update(1)
                    for t in range(NT):
                        gla_prep(t, True)
                        op("act", lambda e, t=t: e.activation(out=Sall[0:64, t], in_=Sst[0:64], func=AF.Copy), [Sst], [Sall])
                        state_update(0)
                        if t in qtiles:
                            af, ab_, ob, zg = bk[0], bk[1], bk[2], bk[3]
                            qdv = qd[:].rearrange("p (h c) -> p h c", h=4)
                            kdv = kd[:].rearrange("p (h c) -> p h c", h=4)
                            for h in range(4):
                                op("pe", lambda e, h=h: e.transpose(pTb[:, h * 128:(h + 1) * 128], qdv[:, h, :], identb[:]), [qd, identb], [pTb])
                                op("pe", lambda e, h=h: e.transpose(pTb[:, (4 + h) * 128:(5 + h) * 128], kdv[:, h, :], identb[:]), [kd, identb], [pTb])
                            op("dve", lambda e: e.tensor_copy(out=qkT[:].rearrange("p a b -> p (a b)"), in_=pTb[:, :]), [pTb], [qkT])
                            for h in range(4):
                                op("pe", lambda e, h=h: e.matmul(af[:, h * 128:(h + 1) * 128], lhsT=qkT[0:64, 4 + h, :], rhs=qkT[0:64, h, :], start=True, stop=True),
                                   [qkT], [af])
                                op("pe", lambda e, h=h: e.matmul(ab_[:, h * 128:(h + 1) * 128], lhsT=qkT[64:128, 4 + h, :], rhs=qkT[64:128, h, :], start=True, stop=True),
                                   [qkT], [ab_])
                            for r, pb in ((0, af), (1, ab_)):
                                op("dve", lambda e, r=r, pb=pb: e.tensor_tensor(
                                    out=Am[:, r], in0=pb[:, 0:512].rearrange("p (h n) -> p h n", h=4),
                                    in1=masks[:, r, :].unsqueeze(1).broadcast_to([128, 4, 128]), op=ALU.mult), [pb, masks], [Am])
                            proj_tok(zg, t, wB, 1056, 1568)
                            for h in range(4):
                                op("pe", lambda e, h=h: e.matmul(ob[:, h * 128:(h + 1) * 128], lhsT=Am[:, 0, h, :], rhs=Vb[:, h * 128:(h + 1) * 128], start=True, stop=False),
                                   [Am, Vb], [ob])
                                op("pe", lambda e, h=h: e.matmul(ob[:, h * 128:(h + 1) * 128], lhsT=Am[:, 1, h, :], rhs=Vb[:, h * 128:(h + 1) * 128], start=False, stop=False),
                                   [Am, Vb], [ob])
                                op("pe", lambda e, h=h, t=t: e.matmul(ob[:, h * 128:(h + 1) * 128], lhsT=qkT[:, h, :], rhs=Sall[:, t, h, :], start=False, stop=True),
                                   [qkT, Sall], [ob])
                            op("act", lambda e: e.activation(out=sqo[:], in_=ob[:, 0:512], func=AF.Square), [ob], [sqo])
                            op("dve", lambda e: e.tensor_reduce(out=so4[:], in_=sqo[:].rearrange("p (h d) -> p h d", h=4), axis=AX.X, op=ALU.add), [sqo], [so4])
                            rstd_from_ss(so4[:, 0:4], so4[:, 0:4], 128, [so4])
                            op("act", lambda e: e.activation(out=eg[:], in_=zg[:, 0:512], func=AF.Exp, scale=-1.0), [zg], [eg])
                            op("dve", lambda e: e.tensor_scalar(out=eg[:], in0=eg[:], scalar1=1.0, scalar2=None, op0=ALU.add), [eg], [eg])
                            op("dve", lambda e: e.reciprocal(out=eg[:], in_=eg[:]), [eg], [eg])
                            op("dve", lambda e: e.tensor_tensor(out=sg[:], in0=zg[:, 0:512], in1=eg[:], op=ALU.mult), [zg, eg], [sg])
                            op("pool", lambda e: e.tensor_tensor(out=sg[:], in0=sg[:], in1=gob[:], op=ALU.mult), [sg, gob], [sg])
                            op("dve", lambda e: e.tensor_tensor(out=on[:].rearrange("p (h d) -> p h d", h=4), in0=ob[:, 0:512].rearrange("p (h d) -> p h d", h=4),
                                                                in1=so4[:].unsqueeze(2).broadcast_to([128, 4, 128]), op=ALU.mult), [ob, so4], [on])
                            op("dve", lambda e: e.tensor_tensor(out=oB[:], in0=on[:], in1=sg[:], op=ALU.mult), [on, sg], [oB])
                            if s == 0 and l == 0 and t in (0, 2, 17):
                                tap("oB%d" % t, oB[:], [128, 512], [oB], BF16)
                            store_oT(oB, 1, t, oTs, pTb)
                    P.barrier()
                if stop_after == "B":
                    break

                with ExitStack() as st:
                    wo = sb("wo", [128, 8, D], BF16, st)
                    load_w(wo, wout_b[l].rearrange("(k p) n -> p k n", p=128), ("wout", l))
                    w3 = [sb("w3_%d" % i, [128, 36, 128], BF16, st) for i in range(2)]
                    oTg = sb("oTg", [128, 12, 512], BF16, st)
                    mTg = sb("mTg", [128, 8, 512], BF16, st)
                    sig = [sb("sig%d" % i, [128, 512], F32, st) for i in range(2)]
                    tz = [sb("tz%d" % i, [128, 512], F32, st) for i in range(3)]
                    gtb = sb("gtb3", [128, 2, D], F32, st)
                    dg = sb("dg3", [128, 128], F32, st)
                    pg = [ps("pg%d" % i, [128, 512], F32, st) for i in range(2)]
                    pp = [ps("pp%d" % i, [128, 512], F32, st) for i in range(2)]
                    py = [ps("py%d" % i, [128, 512], F32, st) for i in range(4)]
                    junk = sb("junk3", [128, D], BF16, st)
                    ytmp = sb("ytmp3", [128, D], F32, st)
                    onesf = sb("onesf3", [128, 128], F32, st)
                    op("dve", lambda e: e.memset(onesf[:], 1.0), [], [onesf])

                    def bcast_vec(dst_ap_fn, dst_buf, kind, j, pbank):
                        for kc in range(8):
                            op("dve", lambda e, kc=kc: e.tensor_scalar(out=dg[:], in0=identf[:], scalar1=dv[:, l, kind, kc, j:j + 1], scalar2=None, op0=ALU.mult),
                               [identf, dv], [dg])
                            op("pe", lambda e, kc=kc: e.matmul(pbank[:, (kc % 4) * 128:(kc % 4 + 1) * 128], lhsT=onesf[:], rhs=dg[:], start=True, stop=True),
                               [onesf, dg], [pbank])
                            if kc % 4 == 3:
                                op("act", lambda e, kc=kc: e.activation(out=dst_ap_fn(kc // 4), in_=pbank[:, 0:512], func=AF.Copy), [pbank], [dst_buf])

                    for jj, j in enumerate((s, 2)):
                        bcast_vec(lambda hf, jj=jj: gtb[:, jj, hf * 512:(hf + 1) * 512], gtb, 2, j, py[0])

                    def post_norm_update(t, ybanks, gtb_row, stat_col):
                        sst = stat_s[t]
                        for hf in range(2):
                            op("act", lambda e, hf=hf: e.activation(out=junk[:, hf * 512:(hf + 1) * 512], in_=ybanks[hf][:, 0:512], func=AF.Square,
                                                                    accum_out=stat[:, t, 2 + hf:3 + hf]), [ybanks[hf]], [junk, sst])
                        op("dve", lambda e: e.tensor_tensor(out=stat[:, t, 2:3], in0=stat[:, t, 2:3], in1=stat[:, t, 3:4], op=ALU.add), [sst], [sst])
                        rstd_from_ss(stat[:, t, 2:3], stat[:, t, 3:4], D, [sst])
                        for hf in range(2):
                            op("dve", lambda e, hf=hf: e.scalar_tensor_tensor(
                                out=ytmp[:, hf * 512:(hf + 1) * 512], in0=ybanks[hf][:, 0:512], scalar=stat[:, t, 3:4],
                                in1=gtb_row[:, hf * 512:(hf + 1) * 512], op0=ALU.mult, op1=ALU.mult), [ybanks[hf], sst, gtb], [ytmp])
                        op("pool", lambda e: e.tensor_tensor(out=xs[:, t, :], in0=xs[:, t, :], in1=ytmp[:], op=ALU.add), [xs_s[t], ytmp], [xs_s[t]])

                    groups = ([] if last else [[0, 1]]) + [list(range(2 + 4 * g, 6 + 4 * g)) for g in range(4)]
                    wi = 0
                    for grp in groups:
                        G = len(grp) * 128
                        c0 = grp[0] * 128
                        dma("sp", oTg[:, :, 0:G], oT_dd[:, :, c0:c0 + G], [oT_slot], [oTg])
                        for fc in range(8):
                            w = w3[wi % 2]
                            wi += 1
                            dma("sp", w[:].rearrange("p a b -> p (a b)"), w3c_b[l, fc], [wsl[("w3", l)]], [w])
                            for z in range(3):
                                g_ = pg[z % 2]
                                p_ = pp[z % 2]
                                for kc in range(8):
                                    op("pe", lambda e, kc=kc, z=z, w=w, g_=g_: e.matmul(g_[:, 0:G], lhsT=w[:, z * 8 + kc, :], rhs=hT[:, kc, c0:c0 + G],
                                                                                      start=(kc == 0), stop=(kc == 7)), [w] + [hT_s[t] for t in grp], [g_])
                                for kc in range(4):
                                    op("pe", lambda e, kc=kc, z=z, w=w, p_=p_: e.matmul(p_[:, 0:G], lhsT=w[:, 24 + z * 4 + kc, :], rhs=oTg[:, z * 4 + kc, 0:G],
                                                                                      start=(kc == 0), stop=(kc == 3)), [w, oTg], [p_])
                                sg_ = sig[z % 2]
                                op("act", lambda e, g_=g_, sg_=sg_: e.activation(out=sg_[:, 0:G], in_=g_[:, 0:G], func=AF.Sigmoid), [g_], [sg_])
                                op("dve", lambda e, z=z, p_=p_, sg_=sg_: e.tensor_tensor(out=tz[z][:, 0:G], in0=p_[:, 0:G], in1=sg_[:, 0:G], op=ALU.mult),
                                   [p_, sg_], [tz[z]])
                            op("pool", lambda e: e.tensor_tensor(out=tz[0][:, 0:G], in0=tz[0][:, 0:G], in1=tz[1][:, 0:G], op=ALU.add), [tz[0], tz[1]], [tz[0]])
                            op("pool", lambda e, fc=fc: e.tensor_tensor(out=mTg[:, fc, 0:G], in0=tz[0][:, 0:G], in1=tz[2][:, 0:G], op=ALU.add), [tz[0], tz[2]], [mTg])
                        for gi, t in enumerate(grp):
                            yb = [py[(gi % 2) * 2], py[(gi % 2) * 2 + 1]]
                            for hf in range(2):
                                for kc in range(8):
                                    op("pe", lambda e, kc=kc, hf=hf, gi=gi: e.matmul(yb[hf][:, 0:512], lhsT=mTg[:, kc, gi * 128:(gi + 1) * 128], rhs=wo[:, kc, hf * 512:(hf + 1) * 512],
                                                                                     start=(kc == 0), stop=(kc == 7)), [mTg, wo], [yb[hf]])
                            post_norm_update(t, yb, gtb[:, 1 if t < 2 else 0, :], 2)
                    if s == 0 and l == 0:
                        tap("x_mid", xs[:], [128, NT, D], xs_s)
                    P.barrier()
                if stop_after == "P3":
                    break

                with ExitStack() as st:
                    phase_norm(l, s, 1, st)
                    P.barrier()
                with ExitStack() as st:
                    w1b = [sb("w1b%d" % i, [128, 8, 512], BF16, st) for i in range(2)]
                    w2b = [sb("w2b%d" % i, [128, 4, D], BF16, st) for i in range(2)]
                    hid = sb("hid", [128, 32, 512], BF16, st)
                    rl = [sb("rl%d" % i, [128, 512], F32, st) for i in range(2)]
                    gtb = sb("gtb4", [128, 2, D], F32, st)
                    dg = sb("dg4", [128, 128], F32, st)
                    pk = [ps("pk%d" % i, [128, 512], F32, st) for i in range(8)]
                    junk = sb("junk4", [128, D], BF16, st)
                    ytmp = sb("ytmp4", [128, D], F32, st)
                    onesf = sb("onesf4", [128, 128], F32, st)
                    op("dve", lambda e: e.memset(onesf[:], 1.0), [], [onesf])
                    for jj, j in enumerate((s, 2)):
                        for kc in range(8):
                            op("dve", lambda e, kc=kc, j=j: e.tensor_scalar(out=dg[:], in0=identf[:], scalar1=dv[:, l, 5, kc, j:j + 1], scalar2=None, op0=ALU.mult),
                               [identf, dv], [dg])
                            op("pe", lambda e, kc=kc: e.matmul(pk[0][:, (kc % 4) * 128:(kc % 4 + 1) * 128], lhsT=onesf[:], rhs=dg[:], start=True, stop=True),
                               [onesf, dg], [pk[0]])
                            if kc % 4 == 3:
                                op("act", lambda e, kc=kc, jj=jj: e.activation(out=gtb[:, jj, (kc // 4) * 512:(kc // 4 + 1) * 512], in_=pk[0][:, 0:512], func=AF.Copy),
                                   [pk[0]], [gtb])
                    groups = ([] if last else [[0, 1]]) + [list(range(2 + 4 * g, 6 + 4 * g)) for g in range(4)]
                    wi = 0
                    for grp in groups:
                        G = len(grp) * 128
                        c0 = grp[0] * 128
                        for hc4 in range(8):
                            w = w1b[wi % 2]
                            wi += 1
                            dma("sp", w[:].rearrange("p a b -> p (a b)"), w1c_b[l, hc4], [wsl[("w1", l)]], [w])
                            for hcc in range(4):
                                hc = hc4 * 4 + hcc
                                pb = pk[hc % 2]
                                for kc in range(8):
                                    op("pe", lambda e, kc=kc, hcc=hcc, w=w, pb=pb: e.matmul(pb[:, 0:G], lhsT=w[:, kc, hcc * 128:(hcc + 1) * 128], rhs=hT[:, kc, c0:c0 + G],
                                                                                          start=(kc == 0), stop=(kc == 7)), [w] + [hT_s[t] for t in grp], [pb])
                                r_ = rl[hc % 2]
                                op("act", lambda e, pb=pb, r_=r_: e.activation(out=r_[:, 0:G], in_=pb[:, 0:G], func=AF.Relu), [pb], [r_])
                                op("dve" if hc % 2 == 0 else "pool", lambda e, hc=hc, r_=r_: e.tensor_tensor(out=hid[:, hc, 0:G], in0=r_[:, 0:G], in1=r_[:, 0:G], op=ALU.mult),
                                   [r_], [hid])
                        for hc4 in range(8):
                            w = w2b[hc4 % 2]
                            dma("sp", w[:].rearrange("p a b -> p (a b)"), w2c_b[l, hc4], [wsl[("w2", l)]], [w])
                            for hcc in range(4):
                                hc = hc4 * 4 + hcc
                                for gi in range(len(grp)):
                                    for hf in range(2):
                                        op("pe", lambda e, hc=hc, hcc=hcc, gi=gi, hf=hf, w=w: e.matmul(
                                            pk[gi * 2 + hf][:, 0:512], lhsT=hid[:, hc, gi * 128:(gi + 1) * 128], rhs=w[:, hcc, hf * 512:(hf + 1) * 512],
                                            start=(hc == 0), stop=(hc == 31)), [hid, w], [pk[gi * 2 + hf]])
                        for gi, t in enumerate(grp):
                            yb = [pk[gi * 2], pk[gi * 2 + 1]]
                            sst = stat_s[t]
                            for hf in range(2):
                                op("act", lambda e, hf=hf, t=t: e.activation(out=junk[:, hf * 512:(hf + 1) * 512], in_=yb[hf][:, 0:512], func=AF.Square,
                                                                             accum_out=stat[:, t, 2 + hf:3 + hf]), [yb[hf]], [junk, sst])
                            op("dve", lambda e, t=t: e.tensor_tensor(out=stat[:, t, 2:3], in0=stat[:, t, 2:3], in1=stat[:, t, 3:4], op=ALU.add), [sst], [sst])
                            rstd_from_ss(stat[:, t, 2:3], stat[:, t, 3:4], D, [sst])
                            jj = 1 if t < 2 else 0
                            for hf in range(2):
                                op("dve", lambda e, hf=hf, t=t, jj=jj: e.scalar_tensor_tensor(
                                    out=ytmp[:, hf * 512:(hf + 1) * 512], in0=yb[hf][:, 0:512], scalar=stat[:, t, 3:4],
                                    in1=gtb[:, jj, hf * 512:(hf + 1) * 512], op0=ALU.mult, op1=ALU.mult), [yb[hf], sst, gtb], [ytmp])
                            op("pool", lambda e, t=t: e.tensor_tensor(out=xs[:, t, :], in0=xs[:, t, :], in1=ytmp[:], op=ALU.add), [xs_s[t], ytmp], [xs_s[t]])
                    if s == 0 and l == 0:
                        tap("x_l0", xs[:], [128, NT, D], xs_s)
                    P.barrier()
            for t in range(2, NT):
                dma("sp", out_d[s, (t - 2) * 128:(t - 1) * 128, :], xs[:, t, :], [xs_s[t]], [out_slot_for(s, 0)])
            P.barrier()
        P.barrier()
    return nc, P.ninst


_out_slots = {}


def out_slot_for(s, t):
    k = (s, t)
    if k not in _out_slots:
        _out_slots[k] = Buf(None, "out%d_%d" % k)
    return _out_slots[k]


def host_consts():
    GRID_W = 64
    ident = np.eye(128, dtype=np.float32)
    j = np.arange(128)[:, None]
    i = np.arange(128)[None, :]
    masks = np.stack([(j <= i), (j >= i)], axis=1).astype(np.float32)
    c = -1.0 / 16.0
    tri = np.stack([(j <= i), (j > i), (j >= i), (j < i)], axis=1).astype(np.float32) * np.float32(c)

    def tables(rot_dim):
        n = 2048
        row = np.repeat(np.arange(n // GRID_W, dtype=np.float32), GRID_W)
        col = np.tile(np.arange(GRID_W, dtype=np.float32), n // GRID_W)
        n_freq = rot_dim // 4
        inv_freq = (np.float32(10000.0) ** (-np.arange(n_freq, dtype=np.float32) / np.float32(n_freq))).astype(np.float32)
        ang = np.concatenate([row[:, None] * inv_freq, col[:, None] * inv_freq], axis=-1).astype(np.float32)
        cs, sn = np.cos(ang).astype(np.float32), np.sin(ang).astype(np.float32)
        half = rot_dim // 2
        out = np.zeros((128, 2, NT, half), np.float32)
        out[:, 0, 0:2, :] = 1.0
        out[:, 0, 2:, :] = cs.reshape(16, 128, half).transpose(1, 0, 2)
        out[:, 1, 2:, :] = sn.reshape(16, 128, half).transpose(1, 0, 2)
        return out
    return ident, masks, tri, tables(64), tables(32)


def make_in_maps(inp, n_cores=8):
    f = lambda a: np.ascontiguousarray(np.asarray(a, dtype=np.float32))
    ident, masks, tri, ropeA, ropeC = host_consts()
    L = 2

    def fm(v, nch):
        return np.ascontiguousarray(np.asarray(v, np.float32).reshape(L, nch, 128).transpose(2, 0, 1))
    badaT = fm(inp["b_ada"], 48)
    gT = np.ascontiguousarray(np.stack([fm(inp[k], 8) for k in ("g_pre_attn", "g_post_attn", "g_pre_mlp", "g_post_mlp")], axis=2))
    gqk = np.concatenate([np.tile(np.asarray(inp["a_g_q"], np.float32), (1, 8)), np.tile(np.asarray(inp["a_g_k"], np.float32), (1, 2))], axis=1)
    gqk = np.ascontiguousarray(np.broadcast_to(gqk[:, None, :], (L, 128, 640)))
    wal = np.zeros((L, 33, 512), np.float32)
    wa = np.asarray(inp["b_w_alpha"], np.float32)
    ba = np.asarray(inp["b_b_alpha"], np.float32)
    wal[:, 0:16, 0:256] = wa[:, 0]
    wal[:, 16:32, 256:512] = wa[:, 1]
    wal[:, 32, 0:256] = ba[:, 0]
    wal[:, 32, 256:512] = ba[:, 1]
    gob = np.ascontiguousarray(np.broadcast_to(np.tile(np.asarray(inp["b_g_out"], np.float32), (1, 4))[:, None, :], (L, 128, 512)))
    cgqT = fm(inp["c_g_q"], 3)
    cgkvT = fm(inp["c_g_kv"], 2)
    shared = {
        "w_ada": f(inp["w_ada"]), "badaT": badaT, "gT": gT, "w_in": f(inp["w_in"]), "gqk": gqk, "wal": wal, "gob": gob,
        "cgqT": cgqT, "cgkvT": cgkvT, "c_w_uq": f(inp["c_w_uq"]), "c_w_ukv": f(inp["c_w_ukv"]), "w_branch": f(inp["w_branch"]),
        "w_out": f(inp["w_out"]), "w_mlp_in": f(inp["w_mlp_in"]), "w_mlp_out": f(inp["w_mlp_out"]),
        "ident": ident, "masks": masks, "tri": tri, "ropeA": ropeA, "ropeC": ropeC,
    }
    x = f(inp["x"])
    ctx = f(inp["ctx"])
    c = f(inp["c"])
    cc = f(inp["c_ctx"])
    maps = []
    for i in range(n_cores):
        cols = np.stack([c[2 * i], c[2 * i + 1], cc], axis=1)
        cT = np.ascontiguousarray(cols.reshape(8, 128, 3).transpose(1, 0, 2))
        m = dict(shared)
        m.update({"x": np.ascontiguousarray(x[2 * i:2 * i + 2]), "ctx": np.ascontiguousarray(ctx[2 * i:2 * i + 2]), "cT": cT})
        maps.append(m)
    return maps


_NC = None


def kernel(**inputs):
    global _NC
    if _NC is None:
        _NC = build()[0]
    maps = make_in_maps(inputs)
    res = run_bass_kernel_spmd(_NC, maps, core_ids=list(range(8)))
    out = np.concatenate([np.asarray(r["out"], dtype=np.float32) for r in res.results], axis=0)
    return out
```
